# Optimizing a Trainium2 kernel written in Bass

```python
import math
import jax, jax.numpy as jnp
from jax import lax
import numpy as np

D_MODEL = 1024
BATCH = 4
SEQ = 8192
DEPTH = 4

HEAD_DIM = 64
MIX_WIDTH = D_MODEL
ATT_HEADS = (3 * MIX_WIDTH) // (8 * HEAD_DIM)
ATT_WIDTH = ATT_HEADS * HEAD_DIM
DN_HEADS = (3 * MIX_WIDTH) // (8 * HEAD_DIM)
DN_WIDTH = DN_HEADS * HEAD_DIM
POOL_WIDTH = MIX_WIDTH - ATT_WIDTH - DN_WIDTH
POOL_WINDOWS = (2, 4, 8, 16)
POOL_GROUPS = len(POOL_WINDOWS)
POOL_GDIM = POOL_WIDTH // POOL_GROUPS
DILATED_GROUPS = ((128, 1), (512, 4), (2048, 16))
ATT_BLOCK = 128
CONV_K = 4
DN_CHUNK = 64
MEM_LEN = 256
X_HEADS = 4
X_HEAD_DIM = D_MODEL // X_HEADS
D_FF = 2816
N_EXPERTS = 8
TOP_K = 2
D_EXPERT = 3584
MOE_BLOCK = 256
N_DENSE = (DEPTH + 1) // 2
N_MOE = DEPTH // 2
DEEPNORM_ALPHA = (2 * DEPTH) ** 0.25
DEEPNORM_BETA = (8 * DEPTH) ** -0.25
LN_EPS = 1e-5
NORM_EPS = 1e-6

OFF_ATT = 0
OFF_DN = OFF_ATT + 3 * ATT_WIDTH
OFF_BETA = OFF_DN + 3 * DN_WIDTH
OFF_DECAY = OFF_BETA + DN_HEADS
OFF_GATE = OFF_DECAY + DN_HEADS
OFF_POOL = OFF_GATE + DN_WIDTH
IN_COLS = OFF_POOL + POOL_WIDTH

kernel_name = "hybrid_dilated_deltanet_pool_moe_trunk"


def _layer_norm(x, g, b):
    xf = x.astype(jnp.float32)
    mu = xf.mean(-1, keepdims=True)
    var = jnp.square(xf - mu).mean(-1, keepdims=True)
    return ((xf - mu) * lax.rsqrt(var + LN_EPS) * g + b).astype(x.dtype)


def _rms_norm(x, w):
    xf = x.astype(jnp.float32)
    return xf * lax.rsqrt(jnp.mean(jnp.square(xf), -1, keepdims=True) + NORM_EPS) * w


def _l2norm(x):
    return x * lax.rsqrt(jnp.sum(jnp.square(x), -1, keepdims=True) + NORM_EPS)


def _swiglu(x, w1, w3, w2):
    return (jax.nn.silu(x @ w1) * (x @ w3)) @ w2


def _alibi_slopes(n):
    return jnp.exp2(-8.0 * jnp.arange(1, n + 1, dtype=jnp.float32) / n)


def _dilated_branch(q, k, v, pos, slopes, window, dilation):
    B, S, H, E = q.shape
    L = S // dilation
    span = window // dilation
    nb = -(-L // ATT_BLOCK)
    Lp = nb * ATT_BLOCK

    def to_sub(t):
        t = t.reshape(B, L, dilation, H, E).transpose(0, 2, 3, 1, 4)
        return jnp.pad(t, ((0, 0), (0, 0), (0, 0), (0, Lp - L), (0, 0)))

    def key_windows(t):
        t = jnp.pad(t, ((0, 0), (0, 0), (0, 0), (ATT_BLOCK, 0), (0, 0)))
        t = t.reshape(B, dilation, H, nb + 1, ATT_BLOCK, E)
        return jnp.concatenate([t[:, :, :, :-1], t[:, :, :, 1:]], axis=4)

    qb = to_sub(q).reshape(B, dilation, H, nb, ATT_BLOCK, E)
    kw = key_windows(to_sub(k))
    vw = key_windows(to_sub(v))

    ps = jnp.pad(pos.reshape(B, L, dilation).transpose(0, 2, 1), ((0, 0), (0, 0), (0, Lp - L)))
    pq = ps.reshape(B, dilation, nb, ATT_BLOCK)
    pkp = jnp.pad(ps, ((0, 0), (0, 0), (ATT_BLOCK, 0))).reshape(B, dilation, nb + 1, ATT_BLOCK)
    pk = jnp.concatenate([pkp[:, :, :-1], pkp[:, :, 1:]], axis=3)

    a = jnp.arange(ATT_BLOCK)[:, None]
    c = jnp.arange(2 * ATT_BLOCK)[None, :]
    rel = ATT_BLOCK + a - c
    key_idx = (jnp.arange(nb)[:, None, None] - 1) * ATT_BLOCK + c[None]
    valid = (rel >= 0) & (rel <= span) & (key_idx >= 0)

    dist = (pq[..., :, None] - pk[..., None, :]).astype(jnp.float32)
    s = jnp.einsum('bdhnqe,bdhnke->bdhnqk', qb, kw) * (E ** -0.5)
    s = s - slopes[:, None, None, None] * dist[:, :, None]
    s = jnp.where(valid, s, -jnp.inf)
    m = s.max(-1, keepdims=True)
    p = jnp.exp(s - m)
    den = p.sum(-1, keepdims=True)
    o = jnp.einsum('bdhnqk,bdhnke->bdhnqe', p, vw)

    def from_sub(t):
        e = t.shape[-1]
        t = t.reshape(B, dilation, H, Lp, e)[:, :, :, :L]
        return t.transpose(0, 3, 1, 2, 4).reshape(B, S, H, e)

    return from_sub(m), from_sub(den), from_sub(o)


def _dilated_attention(q, k, v, pos):
    slopes = _alibi_slopes(q.shape[2])
    parts = [_dilated_branch(q, k, v, pos, slopes, w, d) for (w, d) in DILATED_GROUPS]
    m_all = jnp.max(jnp.stack([p[0] for p in parts], 0), axis=0)
    num = sum(jnp.exp(m - m_all) * o for (m, _, o) in parts)
    den = sum(jnp.exp(m - m_all) * s for (m, s, _) in parts)
    return num / den


def _causal_conv(u, w):
    K = w.shape[0]
    S = u.shape[1]
    up = jnp.pad(u, ((0, 0), (K - 1, 0), (0, 0)))
    return sum(up[:, j:j + S] * w[j] for j in range(K))


def _gated_delta_rule(q, k, v, g, beta):
    B, S, H, DK = q.shape
    DV = v.shape[-1]
    N = S // DN_CHUNK

    def chunks(t):
        return t.reshape(B, N, DN_CHUNK, H, t.shape[-1]).transpose(0, 3, 1, 2, 4)

    q = chunks(q) * (DK ** -0.5)
    k = chunks(k)
    v = chunks(v)
    g = g.reshape(B, N, DN_CHUNK, H).transpose(0, 3, 1, 2)
    beta = beta.reshape(B, N, DN_CHUNK, H).transpose(0, 3, 1, 2)
    G = jnp.cumsum(g, axis=-1)
    idx = jnp.arange(DN_CHUNK)
    causal = idx[:, None] >= idx[None, :]
    strict = idx[:, None] > idx[None, :]
    decay = jnp.exp(jnp.where(causal, G[..., :, None] - G[..., None, :], -jnp.inf))
    kk = jnp.einsum('bhnik,bhnjk->bhnij', k, k)
    a_mat = jnp.where(strict, beta[..., :, None] * kk * decay, 0.0) + jnp.eye(DN_CHUNK, dtype=jnp.float32)
    u = lax.linalg.triangular_solve(a_mat, v * beta[..., None], left_side=True, lower=True, unit_diagonal=True)
    w = lax.linalg.triangular_solve(a_mat, k * (beta * jnp.exp(G))[..., None], left_side=True, lower=True,
                                    unit_diagonal=True)
    qk = jnp.einsum('bhnik,bhnjk->bhnij', q, k) * decay
    q_dec = q * jnp.exp(G)[..., None]
    k_dec = k * jnp.exp(G[..., -1:] - G)[..., None]
    g_tot = jnp.exp(G[..., -1])

    def step(state, inp):
        u_n, w_n, qd_n, kd_n, qk_n, gt_n = inp
        v_new = u_n - jnp.einsum('bhck,bhkv->bhcv', w_n, state)
        o_n = jnp.einsum('bhck,bhkv->bhcv', qd_n, state) + jnp.einsum('bhij,bhjv->bhiv', qk_n, v_new)
        state = gt_n[..., None, None] * state + jnp.einsum('bhck,bhcv->bhkv', kd_n, v_new)
        return state, o_n

    xs = tuple(jnp.moveaxis(t, 2, 0) for t in (u, w, q_dec, k_dec, qk, g_tot))
    state0 = jnp.zeros((B, H, DK, DV), jnp.float32)
    _, o = lax.scan(step, state0, xs)
    return o.transpose(1, 0, 3, 2, 4).reshape(B, S, H, DV)


def _multiscale_pool(u, w_group, scale):
    B, S, _ = u.shape
    uf = u.astype(jnp.float32).reshape(B, S, POOL_GROUPS, POOL_GDIM)
    cs = jnp.pad(jnp.cumsum(uf, axis=1), ((0, 0), (1, 0), (0, 0), (0, 0)))
    t = jnp.arange(S)
    outs = []
    for gi, win in enumerate(POOL_WINDOWS):
        cs_g = cs[:, :, gi]
        lag = jnp.maximum(t + 1 - win, 0)
        cnt = jnp.minimum(t + 1, win).astype(jnp.float32)[None, :, None]
        outs.append((cs_g[:, 1:] - cs_g[:, lag]) / cnt - uf[:, :, gi])
    pooled = jnp.stack(outs, axis=2)
    mixed = jnp.einsum('bsgc,gcd->bsgd', pooled, w_group.astype(jnp.float32))
    return mixed.reshape(B, S, POOL_WIDTH) * scale


def _hybrid_mixer(x, positions, w_in, conv_w, a_log, dt_bias, dn_norm_w, pool_w, pool_scale, w_out):
    B, S, _ = x.shape
    f32 = jnp.float32
    h = x @ w_in
    qa, ka, va = [h[..., OFF_ATT + i * ATT_WIDTH:OFF_ATT + (i + 1) * ATT_WIDTH]
                  .reshape(B, S, ATT_HEADS, HEAD_DIM).astype(f32) for i in range(3)]
    att = _dilated_attention(qa, ka, va, positions).reshape(B, S, ATT_WIDTH)
    qkv = jax.nn.silu(_causal_conv(h[..., OFF_DN:OFF_DN + 3 * DN_WIDTH], conv_w).astype(f32))
    qd, kd, vd = [qkv[..., i * DN_WIDTH:(i + 1) * DN_WIDTH].reshape(B, S, DN_HEADS, HEAD_DIM) for i in range(3)]
    qd = _l2norm(qd)
    kd = _l2norm(kd)
    beta = jax.nn.sigmoid(h[..., OFF_BETA:OFF_BETA + DN_HEADS].astype(f32))
    g = -jnp.exp(a_log.astype(f32)) * jax.nn.softplus(h[..., OFF_DECAY:OFF_DECAY + DN_HEADS].astype(f32)
                                                      + dt_bias.astype(f32))
    od = _gated_delta_rule(qd, kd, vd, g, beta)
    gate = h[..., OFF_GATE:OFF_GATE + DN_WIDTH].astype(f32).reshape(B, S, DN_HEADS, HEAD_DIM)
    dn = (_rms_norm(od, dn_norm_w) * jax.nn.silu(gate)).reshape(B, S, DN_WIDTH)
    pool = _multiscale_pool(h[..., OFF_POOL:OFF_POOL + POOL_WIDTH], pool_w, pool_scale)
    mixed = jnp.concatenate([att.astype(x.dtype), dn.astype(x.dtype), pool.astype(x.dtype)], axis=-1)
    return mixed @ w_out


def _cross_attention(x, mem, wq, wk, wv, wo):
    B, S, D = x.shape
    M = mem.shape[1]
    q = (x @ wq).reshape(B, S, X_HEADS, X_HEAD_DIM)
    k = (mem @ wk).reshape(B, M, X_HEADS, X_HEAD_DIM)
    v = (mem @ wv).reshape(B, M, X_HEADS, X_HEAD_DIM)
    s = jnp.einsum('bshe,bmhe->bhsm', q, k).astype(jnp.float32) * (X_HEAD_DIM ** -0.5)
    p = jax.nn.softmax(s, axis=-1).astype(x.dtype)
    o = jnp.einsum('bhsm,bmhe->bshe', p, v).reshape(B, S, D)
    return o @ wo


def _moe_swiglu(x, router_w, w1, w3, w2):
    B, S, D = x.shape
    T = B * S
    A = T * TOP_K
    xt = x.reshape(T, D)
    logits = (xt @ router_w).astype(jnp.float32)
    top_logit, top_e = lax.top_k(logits, TOP_K)
    gate = jax.nn.softmax(top_logit, axis=-1)
    flat_e = top_e.reshape(-1)
    flat_tok = jnp.repeat(jnp.arange(T, dtype=jnp.int32), TOP_K)
    flat_gate = gate.reshape(-1)
    order = jnp.argsort(flat_e)
    sorted_e = flat_e[order]
    counts = jnp.bincount(flat_e, length=N_EXPERTS)
    padded = (counts + MOE_BLOCK - 1) // MOE_BLOCK * MOE_BLOCK
    start = jnp.cumsum(counts) - counts
    pad_end = jnp.cumsum(padded)
    pad_start = pad_end - padded
    dest = pad_start[sorted_e] + jnp.arange(A, dtype=jnp.int32) - start[sorted_e]
    n_blocks = -(-A // MOE_BLOCK) + N_EXPERTS
    n_slots = n_blocks * MOE_BLOCK
    slot_tok = jnp.zeros((n_slots,), jnp.int32).at[dest].set(flat_tok[order])
    slot_gate = jnp.zeros((n_slots,), jnp.float32).at[dest].set(flat_gate[order])
    block_e = jnp.minimum(jnp.searchsorted(pad_end, jnp.arange(n_blocks, dtype=jnp.int32) * MOE_BLOCK,
                                           side='right'), N_EXPERTS - 1)
    xs = xt[slot_tok].reshape(n_blocks, MOE_BLOCK, D)

    def expert_block(args):
        xb, e = args
        return _swiglu(xb, w1[e], w3[e], w2[e])

    ys = lax.map(expert_block, (xs, block_e))
    y = jnp.zeros((T, D), jnp.float32).at[slot_tok].add(
        ys.reshape(n_slots, D).astype(jnp.float32) * slot_gate[:, None])
    return y.reshape(B, S, D).astype(x.dtype)


def setup_inputs(seed: int = 0) -> dict:
    key = jax.random.key(seed)
    ks = jax.random.split(key, 32)
    f32 = jnp.float32

    def nrm(k, shape, fan_in, mult=1.0):
        return jax.random.normal(k, shape, f32) * (mult * fan_in ** -0.5)

    def gain(k, shape):
        return 1.0 + 0.1 * jax.random.normal(k, shape, f32)

    def bias(k, shape):
        return 0.02 * jax.random.normal(k, shape, f32)

    bd = DEEPNORM_BETA
    x = jax.random.normal(ks[0], (BATCH, SEQ, D_MODEL), f32)
    mem = jax.random.normal(ks[1], (BATCH, MEM_LEN, D_MODEL), f32)
    positions = (jax.random.randint(ks[2], (BATCH, 1), 0, 4096, dtype=jnp.int32)
                 + jnp.arange(SEQ, dtype=jnp.int32)[None, :])
    col_scale = jnp.ones((IN_COLS,), f32)
    col_scale = col_scale.at[OFF_ATT + 2 * ATT_WIDTH:OFF_ATT + 3 * ATT_WIDTH].set(bd)
    col_scale = col_scale.at[OFF_DN + 2 * DN_WIDTH:OFF_DN + 3 * DN_WIDTH].set(bd)
    w_in = nrm(ks[3], (DEPTH, D_MODEL, IN_COLS), D_MODEL) * col_scale
    conv_w = nrm(ks[4], (DEPTH, CONV_K, 3 * DN_WIDTH), CONV_K)
    a_log = jnp.log(jax.random.uniform(ks[5], (DEPTH, DN_HEADS), f32, 1.0, 16.0))
    dt = jnp.exp(jax.random.uniform(ks[6], (DEPTH, DN_HEADS), f32, math.log(1e-3), math.log(1e-1)))
    dt_bias = dt + jnp.log(-jnp.expm1(-dt))
    dn_norm_w = gain(ks[7], (DEPTH, HEAD_DIM))
    pool_w = nrm(ks[8], (DEPTH, POOL_GROUPS, POOL_GDIM, POOL_GDIM), POOL_GDIM)
    pool_scale = gain(ks[9], (DEPTH, POOL_WIDTH))
    w_out = nrm(ks[10], (DEPTH, MIX_WIDTH, D_MODEL), MIX_WIDTH, bd)
    ln_mix_g = gain(ks[11], (DEPTH, D_MODEL))
    ln_mix_b = bias(ks[12], (DEPTH, D_MODEL))
    xq_w = nrm(ks[13], (DEPTH, D_MODEL, D_MODEL), D_MODEL)
    xk_w = nrm(ks[14], (DEPTH, D_MODEL, D_MODEL), D_MODEL)
    xv_w = nrm(ks[15], (DEPTH, D_MODEL, D_MODEL), D_MODEL, bd)
    xo_w = nrm(ks[16], (DEPTH, D_MODEL, D_MODEL), D_MODEL, bd)
    ln_x_g = gain(ks[17], (DEPTH, D_MODEL))
    ln_x_b = bias(ks[18], (DEPTH, D_MODEL))
    ffn_w1 = nrm(ks[19], (N_DENSE, D_MODEL, D_FF), D_MODEL, bd)
    ffn_w3 = nrm(ks[20], (N_DENSE, D_MODEL, D_FF), D_MODEL, bd)
    ffn_w2 = nrm(ks[21], (N_DENSE, D_FF, D_MODEL), D_FF, bd)
    router_w = nrm(ks[22], (N_MOE, D_MODEL, N_EXPERTS), D_MODEL)
    moe_w1 = nrm(ks[23], (N_MOE, N_EXPERTS, D_MODEL, D_EXPERT), D_MODEL, bd)
    moe_w3 = nrm(ks[24], (N_MOE, N_EXPERTS, D_MODEL, D_EXPERT), D_MODEL, bd)
    moe_w2 = nrm(ks[25], (N_MOE, N_EXPERTS, D_EXPERT, D_MODEL), D_EXPERT, bd)
    ln_ffn_g = gain(ks[26], (DEPTH, D_MODEL))
    ln_ffn_b = bias(ks[27], (DEPTH, D_MODEL))
    return {"x": x, "mem": mem, "positions": positions, "w_in": w_in, "conv_w": conv_w,
            "a_log": a_log, "dt_bias": dt_bias, "dn_norm_w": dn_norm_w, "pool_w": pool_w,
            "pool_scale": pool_scale, "w_out": w_out, "ln_mix_g": ln_mix_g, "ln_mix_b": ln_mix_b,
            "xq_w": xq_w, "xk_w": xk_w, "xv_w": xv_w, "xo_w": xo_w, "ln_x_g": ln_x_g, "ln_x_b": ln_x_b,
            "ffn_w1": ffn_w1, "ffn_w3": ffn_w3, "ffn_w2": ffn_w2, "router_w": router_w,
            "moe_w1": moe_w1, "moe_w3": moe_w3, "moe_w2": moe_w2,
            "ln_ffn_g": ln_ffn_g, "ln_ffn_b": ln_ffn_b}


def reference(x, mem, positions, w_in, conv_w, a_log, dt_bias, dn_norm_w, pool_w, pool_scale, w_out,
              ln_mix_g, ln_mix_b, xq_w, xk_w, xv_w, xo_w, ln_x_g, ln_x_b, ffn_w1, ffn_w3, ffn_w2,
              router_w, moe_w1, moe_w3, moe_w2, ln_ffn_g, ln_ffn_b):
    for l in range(DEPTH):
        y = _hybrid_mixer(x, positions, w_in[l], conv_w[l], a_log[l], dt_bias[l], dn_norm_w[l],
                          pool_w[l], pool_scale[l], w_out[l])
        x = _layer_norm(DEEPNORM_ALPHA * x + y, ln_mix_g[l], ln_mix_b[l])
        y = _cross_attention(x, mem, xq_w[l], xk_w[l], xv_w[l], xo_w[l])
        x = _layer_norm(DEEPNORM_ALPHA * x + y, ln_x_g[l], ln_x_b[l])
        if l % 2 == 0:
            j = l // 2
            y = _swiglu(x, ffn_w1[j], ffn_w3[j], ffn_w2[j])
        else:
            j = l // 2
            y = _moe_swiglu(x, router_w[j], moe_w1[j], moe_w3[j], moe_w2[j])
        x = _layer_norm(DEEPNORM_ALPHA * x + y, ln_ffn_g[l], ln_ffn_b[l])
    return x
```

```python
import numpy as np
import concourse.bass as bass
import concourse.mybir as mybir
from concourse.bass_utils import run_bass_kernel_spmd

F32 = mybir.dt.float32
BF16 = mybir.dt.bfloat16
I32 = mybir.dt.int32
AF = mybir.ActivationFunctionType
ALU = mybir.AluOpType
AX = mybir.AxisListType


class Buf:
    def __init__(self, t, name=""):
        self.t = t
        self.name = name
        self.w = None
        self.r = {}
        self.excl = False

    def __getitem__(self, idx):
        return self.t[idx]


class _Eng:
    def __init__(self, name, eng, sem):
        self.name = name
        self.eng = eng
        self.sem = sem
        self.count = 0
        self.waited = {}
        self.q = []


class KB:
    NDSEM = 24

    def __init__(self, nc):
        self.nc = nc
        self.sems = {}
        self.engs = {}
        for n in ("tensor", "vector", "scalar", "gpsimd", "sync"):
            s = nc.alloc_semaphore("s_" + n)
            self.sems["s_" + n] = s
            self.engs[n] = _Eng(n, getattr(nc, n), s)
        self.dsem = []
        for i in range(self.NDSEM):
            s = nc.alloc_semaphore("d%d" % i)
            self.sems["d%d" % i] = s
            self.dsem.append(["d%d" % i, s, 0])
        self.dnext = 0
        self._stack = []

    def sb(self, shape, dt, name=None):
        g = self.nc.sbuf_tensor(name, list(shape), dt) if name else self.nc.sbuf_tensor(list(shape), dt)
        t = g.__enter__()
        self._stack.append(g)
        return Buf(t, name or "")

    def ps(self, shape, dt=F32, name=None):
        g = self.nc.psum_tensor(name, list(shape), dt) if name else self.nc.psum_tensor(list(shape), dt)
        t = g.__enter__()
        self._stack.append(g)
        b = Buf(t, name or "")
        b.excl = True
        return b

    def mark(self):
        return len(self._stack)

    def barrier(self):
        for en, E in self.engs.items():
            for n2, X in self.engs.items():
                if n2 != en:
                    self._wait(E, "s_" + n2, X.count)
            for d in self.dsem:
                self._wait(E, d[0], d[2])

    def release(self, mark):
        if len(self._stack) > mark and mark > 0:
            self.barrier()
        while len(self._stack) > mark:
            g = self._stack.pop()
            g.__exit__(None, None, None)

    def _wait(self, E, key, val):
        if val <= 0:
            return
        if E.waited.get(key, 0) >= val:
            return
        E.q.append(("w", self.sems[key], val))
        E.waited[key] = val

    def _deps(self, E, reads, writes):
        for b in reads:
            if b.w is not None:
                self._wait(E, b.w[0], b.w[1])
            if b.excl:
                for k, v in b.r.items():
                    if k != "s_" + E.name:
                        self._wait(E, k, v)
        for b in writes:
            if b.w is not None:
                self._wait(E, b.w[0], b.w[1])
            for k, v in b.r.items():
                self._wait(E, k, v)

    def _commit(self, tok, reads, writes):
        for b in reads:
            if b.r.get(tok[0], 0) < tok[1]:
                b.r[tok[0]] = tok[1]
        for b in writes:
            b.w = tok
            b.r = {}

    def op(self, en, fn, reads=(), writes=()):
        E = self.engs[en]
        self._deps(E, reads, writes)
        E.count += 1
        E.q.append(("i", fn, E.sem, 1))
        ins = None
        tok = ("s_" + en, E.count)
        self._commit(tok, reads, writes)
        return ins

    def mm(self, out_ap, terms, reads=(), writes=()):
        E = self.engs["tensor"]
        self._deps(E, reads, writes)
        n = len(terms)
        ins = None
        for i, (l, r) in enumerate(terms):
            f = (lambda e, l=l, r=r, i=i: e.matmul(out_ap, l, r, start=(i == 0), stop=(i == n - 1)))
            if i == n - 1:
                E.q.append(("i", f, E.sem, 1))
            else:
                E.q.append(("i", f, None, 0))
        E.count += 1
        tok = ("s_tensor", E.count)
        self._commit(tok, reads, writes)
        return ins

    def transpose(self, out_ap, in_ap, ident_ap, reads=(), writes=()):
        return self.op("tensor", lambda e: e.transpose(out_ap, in_ap, ident_ap), reads, writes)

    def dma(self, qn, out_ap, in_ap, reads=(), writes=(), **kw):
        E = self.engs[qn]
        self._deps(E, reads, writes)
        d = self.dsem[self.dnext]
        self.dnext = (self.dnext + 1) % self.NDSEM
        self._wait(E, d[0], d[2])
        E.q.append(("i", (lambda e: e.dma_start(out=out_ap, in_=in_ap, **kw)), d[1], 16))
        ins = None
        d[2] += 16
        tok = (d[0], d[2])
        self._commit(tok, reads, writes)
        return ins

    def finish(self, bufs):
        E = self.engs["sync"]
        for b in bufs:
            if b.w is not None:
                self._wait(E, b.w[0], b.w[1])
        for n in ("tensor", "vector", "scalar", "gpsimd"):
            X = self.engs[n]
            self._wait(E, "s_" + n, X.count)
        for d in self.dsem:
            self._wait(E, d[0], d[2])
        self.emit()
        self.release(0)

    def emit(self):
        def replay(E):
            def f(eng):
                for it in E.q:
                    if it[0] == "w":
                        eng.wait_ge(it[1], it[2])
                    else:
                        ins = it[1](eng)
                        if it[2] is not None:
                            ins.then_inc(it[2], it[3])
            return f
        with self.nc.Block() as block:
            block.sync(replay(self.engs["sync"]))
            block.tensor(replay(self.engs["tensor"]))
            block.vector(replay(self.engs["vector"]))
            block.scalar(replay(self.engs["scalar"]))
            block.gpsimd(replay(self.engs["gpsimd"]))


def run(nc, in_maps, n=8, trace=False):
    res = run_bass_kernel_spmd(nc, in_maps, core_ids=list(range(n)), trace=trace)
    return res


D = 1024
ALPHA = 8.0 ** 0.25
LN_EPS = 1e-5
NTOK = 4096
SEQ = 8192
DFF = 2816
DEXP = 3584
NEXP = 8


def mk_consts(k):
    c = {}
    identf = k.sb([128, 128], F32, "identf")
    k.op("gpsimd", lambda e: e.memset(identf[:], 1.0), writes=[identf])
    k.op("gpsimd", lambda e: e.affine_select(identf[:], identf[:], pattern=[[-1, 128]], compare_op=ALU.is_equal,
                                             fill=0.0, base=0, channel_multiplier=1), reads=[identf], writes=[identf])
    identb = k.sb([128, 128], BF16, "identb")
    k.op("vector", lambda e: e.tensor_copy(identb[:], identf[:]), reads=[identf], writes=[identb])
    eps = k.sb([128, 1], F32, "epsT")
    k.op("gpsimd", lambda e: e.memset(eps[:], LN_EPS), writes=[eps])
    c["identf"] = identf
    c["identb"] = identb
    c["eps"] = eps
    return c


def load_w(k, dst, dst_ap3, src_ap3, stage, cast_eng="gpsimd", q="sync"):
    p, a, n = src_ap3.shape
    sv = stage[:, 0:a * n].rearrange("p (a n) -> p a n", a=a)
    k.dma(q, sv, src_ap3, writes=[stage])
    k.op(cast_eng, lambda e: e.tensor_copy(dst_ap3, sv), reads=[stage], writes=[dst])


def layernorm(k, z, G, Bt, out, junk, st, eps):
    k.op("vector", lambda e: e.reduce_sum(st[:, 0:1], z[:], axis=AX.X), reads=[z], writes=[st])
    k.op("vector", lambda e: e.tensor_scalar_mul(st[:, 1:2], st[:, 0:1], -1.0 / D), reads=[st], writes=[st])
    k.op("vector", lambda e: e.tensor_scalar_add(z[:], z[:], st[:, 1:2]), reads=[z, st], writes=[z])
    k.op("vector", lambda e: e.memset(st[:, 2:3], 0.0), reads=[], writes=[st])
    k.op("scalar", lambda e: e.activation(junk[:], z[:], AF.Square, accum_out=st[:, 2:3]), reads=[z, st], writes=[junk, st])
    k.op("scalar", lambda e: e.activation(st[:, 3:4], st[:, 2:3], AF.Sqrt, bias=eps[:, 0:1], scale=1.0 / D),
         reads=[st, eps], writes=[st])
    k.op("vector", lambda e: e.reciprocal(st[:, 4:5], st[:, 3:4]), reads=[st], writes=[st])
    k.op("vector", lambda e: e.scalar_tensor_tensor(out[:], z[:], st[:, 4:5], G[:], op0=ALU.mult, op1=ALU.mult),
         reads=[z, st, G], writes=[out])
    k.op("gpsimd", lambda e: e.tensor_tensor(out[:], out[:], Bt[:], op=ALU.add), reads=[out, Bt], writes=[out])


def build_rest(moe, dbg_stop=9):
    nc = bass.Bass("TRN2", target_bir_lowering=False)
    dt = nc.dram_tensor
    xin = dt("xin", [NTOK, D], F32, kind="ExternalInput").ap()
    mixT = dt("mixT", [D, NTOK], BF16, kind="ExternalInput").ap()
    memT = dt("memT", [D, 256], F32, kind="ExternalInput").ap()
    wn = {}
    for n in ("w_out", "xq", "xk", "xv", "xo"):
        wn[n] = dt(n, [D, D], F32, kind="ExternalInput").ap()
    lnp = dt("lnp", [6, D], F32, kind="ExternalInput").ap()
    if moe:
        rw = dt("rw", [128, 8, NEXP], F32, kind="ExternalInput").ap()
        w1 = dt("w1", [NEXP, D, DEXP], F32, kind="ExternalInput").ap()
        w3 = dt("w3", [NEXP, D, DEXP], F32, kind="ExternalInput").ap()
        w2 = dt("w2", [NEXP, DEXP, D], F32, kind="ExternalInput").ap()
        nexp, dff = NEXP, DEXP
    else:
        w1 = dt("w1", [1, D, DFF], F32, kind="ExternalInput").ap()
        w3 = dt("w3", [1, D, DFF], F32, kind="ExternalInput").ap()
        w2 = dt("w2", [1, DFF, D], F32, kind="ExternalInput").ap()
        nexp, dff = 1, DFF
    xout = dt("xout", [NTOK, D], F32, kind="ExternalOutput").ap()
    X2d = dt("X2s", [NTOK, D], F32).ap()
    X2Td = dt("X2Ts", [D, NTOK], BF16).ap()
    k = KB(nc)
    X2 = Buf(X2d, "X2"); X2T = Buf(X2Td, "X2T"); XO = Buf(xout, "xout")
    c = mk_consts(k)
    identf, identb, eps = c["identf"], c["identb"], c["eps"]
    base_mark = k.mark()

    LN = [None] * 6
    for i in (4, 5):
        LN[i] = k.sb([128, D], F32, "ln%d" % i)
        k.dma("gpsimd", LN[i][:], lnp[i:i + 1, :].broadcast_to([128, D]), writes=[LN[i]])
    st = k.sb([128, 8], F32, "st")
    junk = k.sb([128, D], F32, "junk")
    if moe:
        gT = k.sb([8, NTOK], F32, "gT")
        sel = k.sb([8, NEXP, 128], F32, "sel")
        onesr = k.sb([8, 128], F32, "onesr")
        k.op("gpsimd", lambda e: e.memset(onesr[:], 1.0), writes=[onesr])
        for ee in (range(NEXP) if dbg_stop > -2 else []):
            k.op("vector", lambda e, ee=ee: e.tensor_scalar_mul(sel[:, ee, :], onesr[:], identf[0:8, ee:ee + 1]), reads=[onesr, identf], writes=[sel])
    partA_mark = k.mark()

    for i in range(4):
        LN[i] = k.sb([128, D], F32, "ln%d" % i)
        k.dma("gpsimd", LN[i][:], lnp[i:i + 1, :].broadcast_to([128, D]), writes=[LN[i]])
    stage = k.sb([128, 8 * 512], F32, "stage")
    Wb = {}
    for n in ("w_out", "xq", "xo", "xk", "xv"):
        Wb[n] = k.sb([128, 8, D], BF16, "W" + n)
        src = wn[n].rearrange("(kc p) n -> p kc n", p=128)
        for hf in range(2):
            load_w(k, Wb[n], Wb[n][:, :, hf * 512:(hf + 1) * 512], src[:, :, hf * 512:(hf + 1) * 512], stage)
    memTb = k.sb([128, 8, 256], BF16, "memTb")
    load_w(k, memTb, memTb[:], memT.rearrange("(kc p) m -> p kc m", p=128), stage)
    if moe:
        Wr = k.sb([128, 8, NEXP], F32, "Wr")
        if dbg_stop > -3:
            k.dma("sync", Wr[:], rw, writes=[Wr])
    PB = [k.ps([128, 512], F32, "PB%d" % i) for i in range(6)]
    PT = k.ps([128, 1024], BF16, "PT")
    PR = k.ps([128, 512], F32, "PR")
    KT = k.sb([128, 8, 256], BF16, "KT")
    Vb = k.sb([128, 2, D], BF16, "Vb")
    for cc in range(8):
        pb = PB[cc % 2]
        k.mm(pb[:, 0:256], [(Wb["xk"][:, kc, cc * 128:(cc + 1) * 128], memTb[:, kc, :]) for kc in range(8)],
             reads=[Wb["xk"], memTb], writes=[pb])
        k.op("vector", lambda e, pb=pb, cc=cc: e.tensor_copy(KT[:, cc, :], pb[:, 0:256]), reads=[pb], writes=[KT])
    for mc in range(2):
        for hf in range(2):
            pb = PB[2 + hf]
            k.mm(pb[:], [(memTb[:, kc, mc * 128:(mc + 1) * 128], Wb["xv"][:, kc, hf * 512:(hf + 1) * 512]) for kc in range(8)],
                 reads=[Wb["xv"], memTb], writes=[pb])
            k.op("vector", lambda e, pb=pb, mc=mc, hf=hf: e.tensor_copy(Vb[:, mc, hf * 512:(hf + 1) * 512], pb[:]),
                 reads=[pb], writes=[Vb])

    xt = [k.sb([128, D], F32, "xt%d" % i) for i in range(2)]
    mT = [k.sb([128, 8, 128], BF16, "mT%d" % i) for i in range(2)]
    z = k.sb([128, D], F32, "z")
    x1 = k.sb([128, D], F32, "x1")
    x1b = k.sb([128, D], BF16, "x1b")
    x1T = k.sb([128, 8, 128], BF16, "x1T")
    qTb = k.sb([128, 8, 128], BF16, "qTb")
    pex = k.sb([128, 4, 256], F32, "pex")
    pn = k.sb([128, 4, 256], BF16, "pn")
    pT = k.sb([128, 8, 128], BF16, "pT")
    oTb = k.sb([128, 8, 128], BF16, "oTb")
    sm = k.sb([128, 16], F32, "sm")
    x2 = k.sb([128, D], F32, "x2")
    x2Tb = k.sb([128, 8, 128], BF16, "x2Tb")
    if moe:
        x2Tf = k.sb([128, 8, 128], F32, "x2Tf")
        lg = k.sb([128, 64], F32, "lg")
    mixTv = mixT.rearrange("(kc p) t -> p kc t", p=128)
    X2Tv = X2Td.rearrange("(kc p) t -> p kc t", p=128)
    NT = NTOK // 128
    for ti in range(NT):
        tok = slice(ti * 128, (ti + 1) * 128)
        xb = xt[ti % 2]; mb = mT[ti % 2]
        k.dma("sync", xb[:], xin[tok, :], writes=[xb])
        k.dma("sync", mb[:], mixTv[:, :, tok], writes=[mb])
        for hf in range(2):
            hs = slice(hf * 512, (hf + 1) * 512)
            k.mm(PB[hf][:], [(mb[:, kc, :], Wb["w_out"][:, kc, hs]) for kc in range(8)], reads=[mb, Wb["w_out"]], writes=[PB[hf]])
            k.op("vector", lambda e, hs=hs, hf=hf, xb=xb: e.scalar_tensor_tensor(z[:, hs], xb[:, hs], ALPHA, PB[hf][:], op0=ALU.mult, op1=ALU.add),
                 reads=[xb, PB[hf]], writes=[z])
        layernorm(k, z, LN[0], LN[1], x1, junk, st, eps)
        k.op("scalar", lambda e: e.copy(x1b[:], x1[:]), reads=[x1], writes=[x1b])
        for kc in range(8):
            k.transpose(PT[:, kc * 128:(kc + 1) * 128], x1b[:, kc * 128:(kc + 1) * 128], identb[:], reads=[x1b, identb], writes=[PT])
        k.op("vector", lambda e: e.tensor_copy(x1T[:].rearrange("p a b -> p (a b)"), PT[:]), reads=[PT], writes=[x1T])
        for cc in range(8):
            pb = PB[2 + cc // 4]
            k.mm(pb[:, (cc % 4) * 128:(cc % 4 + 1) * 128], [(Wb["xq"][:, kc, cc * 128:(cc + 1) * 128], x1T[:, kc, :]) for kc in range(8)],
                 reads=[Wb["xq"], x1T], writes=[pb])
        for j in range(2):
            k.op("scalar", lambda e, j=j: e.activation(qTb[:, j * 4:(j + 1) * 4, :].rearrange("p a b -> p (a b)"), PB[2 + j][:], AF.Copy, scale=1.0 / 16),
                 reads=[PB[2 + j]], writes=[qTb])
        for h in range(4):
            pb = PB[4 + h // 2]
            k.mm(pb[:, (h % 2) * 256:(h % 2 + 1) * 256], [(qTb[:, 2 * h + j, :], KT[:, 2 * h + j, :]) for j in range(2)],
                 reads=[qTb, KT], writes=[pb])
        for j in range(2):
            k.op("vector", lambda e, j=j: e.tensor_reduce(sm[:, 2 * j:2 * j + 2], PB[4 + j][:].rearrange("p (a b) -> p a b", a=2), axis=AX.X, op=ALU.max),
                 reads=[PB[4 + j]], writes=[sm])
        k.op("vector", lambda e: e.tensor_scalar_mul(sm[:, 4:8], sm[:, 0:4], -1.0), reads=[sm], writes=[sm])
        k.op("vector", lambda e: e.memset(sm[:, 8:12], 0.0), writes=[sm])
        for h in range(4):
            pb = PB[4 + h // 2]
            k.op("scalar", lambda e, h=h, pb=pb: e.activation(pex[:, h, :], pb[:, (h % 2) * 256:(h % 2 + 1) * 256], AF.Exp, bias=sm[:, 4 + h:5 + h], scale=1.0,
                                                        accum_out=sm[:, 8 + h:9 + h]), reads=[pb, sm], writes=[pex, sm])
        k.op("vector", lambda e: e.reciprocal(sm[:, 12:16], sm[:, 8:12]), reads=[sm], writes=[sm])
        for h in range(4):
            k.op("gpsimd", lambda e, h=h: e.tensor_scalar_mul(pn[:, h, :], pex[:, h, :], sm[:, 12 + h:13 + h]), reads=[pex, sm], writes=[pn])
        for h in range(4):
            for mc in range(2):
                i8 = 2 * h + mc
                k.transpose(PT[:, i8 * 128:(i8 + 1) * 128], pn[:, h, mc * 128:(mc + 1) * 128], identb[:], reads=[pn, identb], writes=[PT])
        k.op("vector", lambda e: e.tensor_copy(pT[:].rearrange("p a b -> p (a b)"), PT[:]), reads=[PT], writes=[pT])
        for ec in range(8):
            h = ec // 2
            pb = PB[2 + ec // 4]
            k.mm(pb[:, (ec % 4) * 128:(ec % 4 + 1) * 128], [(Vb[:, mc, ec * 128:(ec + 1) * 128], pT[:, 2 * h + mc, :]) for mc in range(2)],
                 reads=[Vb, pT], writes=[pb])
        for j in range(2):
            k.op("scalar", lambda e, j=j: e.copy(oTb[:, j * 4:(j + 1) * 4, :].rearrange("p a b -> p (a b)"), PB[2 + j][:]),
                 reads=[PB[2 + j]], writes=[oTb])
        for hf in range(2):
            hs = slice(hf * 512, (hf + 1) * 512)
            k.mm(PB[hf][:], [(oTb[:, ec, :], Wb["xo"][:, ec, hs]) for ec in range(8)], reads=[oTb, Wb["xo"]], writes=[PB[hf]])
            k.op("vector", lambda e, hs=hs, hf=hf: e.scalar_tensor_tensor(z[:, hs], x1[:, hs], ALPHA, PB[hf][:], op0=ALU.mult, op1=ALU.add),
                 reads=[x1, PB[hf]], writes=[z])
        layernorm(k, z, LN[2], LN[3], x2, junk, st, eps)
        k.dma("gpsimd", X2d[tok, :], x2[:], reads=[x2], writes=[X2])
        for kc in range(8):
            pb = PB[4 + kc // 4]
            k.transpose(pb[:, (kc % 4) * 128:(kc % 4 + 1) * 128], x2[:, kc * 128:(kc + 1) * 128], identf[:], reads=[x2, identf], writes=[pb])
        for j in range(2):
            k.op("vector", lambda e, j=j: e.tensor_copy(x2Tb[:, j * 4:(j + 1) * 4, :].rearrange("p a b -> p (a b)"), PB[4 + j][:]),
                 reads=[PB[4 + j]], writes=[x2Tb])
            if moe and dbg_stop > -1:
                k.op("scalar", lambda e, j=j: e.copy(x2Tf[:, j * 4:(j + 1) * 4, :].rearrange("p a b -> p (a b)"), PB[4 + j][:]),
                     reads=[PB[4 + j]], writes=[x2Tf])
        k.dma("gpsimd", X2Tv[:, :, tok], x2Tb[:], reads=[x2Tb], writes=[X2T])
        if moe and dbg_stop > 0:
            k.mm(PR[:, 0:8], [(x2Tf[:, kc, :], Wr[:, kc, :]) for kc in range(8)], reads=[x2Tf, Wr], writes=[PR])
            L = lg
            k.op("vector", lambda e: e.tensor_copy(L[:, 0:8], PR[:, 0:8]), reads=[PR], writes=[L])
            k.op("vector", lambda e: e.tensor_reduce(L[:, 8:9], L[:, 0:8], axis=AX.X, op=ALU.max), reads=[L], writes=[L])
            k.op("vector", lambda e: e.tensor_scalar(L[:, 16:24], L[:, 0:8], L[:, 8:9], None, op0=ALU.is_equal), reads=[L], writes=[L])
            k.op("vector", lambda e: e.scalar_tensor_tensor(L[:, 24:32], L[:, 16:24], -1e30, L[:, 0:8], op0=ALU.mult, op1=ALU.add), reads=[L], writes=[L])
            k.op("vector", lambda e: e.tensor_reduce(L[:, 9:10], L[:, 24:32], axis=AX.X, op=ALU.max), reads=[L], writes=[L])
            k.op("vector", lambda e: e.tensor_scalar(L[:, 32:40], L[:, 24:32], L[:, 9:10], None, op0=ALU.is_equal), reads=[L], writes=[L])
            k.op("vector", lambda e: e.tensor_tensor(L[:, 10:11], L[:, 9:10], L[:, 8:9], op=ALU.subtract), reads=[L], writes=[L])
            k.op("scalar", lambda e: e.activation(L[:, 11:12], L[:, 10:11], AF.Exp), reads=[L], writes=[L])
            k.op("vector", lambda e: e.tensor_scalar_add(L[:, 12:13], L[:, 11:12], 1.0), reads=[L], writes=[L])
            k.op("vector", lambda e: e.reciprocal(L[:, 13:14], L[:, 12:13]), reads=[L], writes=[L])
            k.op("vector", lambda e: e.tensor_tensor(L[:, 14:15], L[:, 11:12], L[:, 13:14], op=ALU.mult), reads=[L], writes=[L])
            k.op("vector", lambda e: e.tensor_scalar_mul(L[:, 40:48], L[:, 16:24], L[:, 13:14]), reads=[L], writes=[L])
            k.op("vector", lambda e: e.scalar_tensor_tensor(L[:, 48:56], L[:, 32:40], L[:, 14:15], L[:, 40:48], op0=ALU.mult, op1=ALU.add), reads=[L], writes=[L])
            k.transpose(PR[0:8, 128:256], L[:, 48:56], identf[:], reads=[L, identf], writes=[PR])
            k.op("vector", lambda e, tok=tok: e.tensor_copy(gT[:, tok], PR[0:8, 128:256]), reads=[PR], writes=[gT])
    k.release(partA_mark)

    PB = [k.ps([128, 512], F32, "QB%d" % i) for i in range(8)]
    stage = k.sb([128, 8 * 512], F32, "stageB")
    W1c = [k.sb([128, 8, 512], BF16, "W1c%d" % i) for i in range(2)]
    W3c = [k.sb([128, 8, 512], BF16, "W3c%d" % i) for i in range(2)]
    W2c = [k.sb([128, 4, D], BF16, "W2c%d" % i) for i in range(2)]
    x2Th = k.sb([128, 8, 1024], BF16, "x2Th")
    yt = k.sb([128, 8, D], F32, "yacc")
    yacc = [Buf(yt.t, "yacc%d" % i) for i in range(8)]
    hs1 = [k.sb([128, 512], F32, "hs1_%d" % i) for i in range(2)]
    hs2 = [k.sb([128, 512], F32, "hs2_%d" % i) for i in range(2)]
    aT = [k.sb([128, 4, 512], BF16, "aT%d" % i) for i in range(2)]
    xr = k.sb([128, D], F32, "xr")
    zz = k.sb([128, D], F32, "zz")
    ob = k.sb([128, D], F32, "ob")
    groups = []
    f0 = 0
    while f0 < dff:
        fs = min(512, dff - f0)
        groups.append((f0, fs))
        f0 += fs
    gi = 0
    for half in (range(4) if dbg_stop > 1 else []):
        k.dma("sync", x2Th[:], X2Tv[:, :, half * 1024:(half + 1) * 1024], reads=[X2T], writes=[x2Th])
        first = True
        for ex in range(nexp):
            for (f0, fs) in groups:
                nfc = fs // 128
                a1 = W1c[gi % 2]; a3 = W3c[gi % 2]; a2 = W2c[gi % 2]
                load_w(k, a1, a1[:, :, 0:fs], w1[ex].rearrange("(kc p) f -> p kc f", p=128)[:, :, f0:f0 + fs], stage, cast_eng="gpsimd", q="sync")
                load_w(k, a3, a3[:, :, 0:fs], w3[ex].rearrange("(kc p) f -> p kc f", p=128)[:, :, f0:f0 + fs], stage, cast_eng="gpsimd", q="sync")
                load_w(k, a2, a2[:, 0:nfc, :], w2[ex][f0:f0 + fs, :].rearrange("(fi p) c -> p fi c", p=128), stage, cast_eng="gpsimd", q="sync")
                for mt in range(2):
                    ts_ = slice(mt * 512, (mt + 1) * 512)
                    at = aT[mt % 2]
                    if moe:
                        gtok = slice(half * 1024 + mt * 512, half * 1024 + (mt + 1) * 512)
                        k.mm(PB[6][:], [(sel[:, ex, :], gT[:, gtok])], reads=[sel, gT], writes=[PB[6]])
                    for fi in range(nfc):
                        p1 = PB[fi % 2]; p3 = PB[2 + fi % 2]
                        k.mm(p1[:], [(a1[:, kc, fi * 128:(fi + 1) * 128], x2Th[:, kc, ts_]) for kc in range(8)], reads=[a1, x2Th], writes=[p1])
                        k.mm(p3[:], [(a3[:, kc, fi * 128:(fi + 1) * 128], x2Th[:, kc, ts_]) for kc in range(8)], reads=[a3, x2Th], writes=[p3])
                        s1 = hs1[fi % 2]; s2 = hs2[fi % 2]
                        k.op("scalar", lambda e, s1=s1, p1=p1: e.activation(s1[:], p1[:], AF.Silu), reads=[p1], writes=[s1])
                        if moe:
                            k.op("vector", lambda e, s1=s1, s2=s2, p3=p3: e.tensor_tensor(s2[:], s1[:], p3[:], op=ALU.mult), reads=[s1, p3], writes=[s2])
                            k.op("vector", lambda e, s2=s2, at=at, fi=fi: e.tensor_tensor(at[:, fi, :], s2[:], PB[6][:], op=ALU.mult), reads=[s2, PB[6]], writes=[at])
                        else:
                            k.op("vector", lambda e, s1=s1, at=at, p3=p3, fi=fi: e.tensor_tensor(at[:, fi, :], s1[:], p3[:], op=ALU.mult), reads=[s1, p3], writes=[at])
                    for sub in range(4):
                        tl = mt * 4 + sub
                        for hf in range(2):
                            pb = PB[4 + hf]
                            hs = slice(hf * 512, (hf + 1) * 512)
                            k.mm(pb[:], [(at[:, fi, sub * 128:(sub + 1) * 128], a2[:, fi, hs]) for fi in range(nfc)], reads=[at, a2], writes=[pb])
                            if first:
                                k.op("gpsimd" if False else "vector", lambda e, tl=tl, hs=hs, pb=pb: e.tensor_copy(yt[:, tl, hs], pb[:]), reads=[pb], writes=[yacc[tl]])
                            else:
                                k.op("vector", lambda e, tl=tl, hs=hs, pb=pb: e.tensor_tensor(yt[:, tl, hs], yt[:, tl, hs], pb[:], op=ALU.add), reads=[pb, yacc[tl]], writes=[yacc[tl]])
                first = False
                gi += 1
        for tl in range(8):
            tok = slice(half * 1024 + tl * 128, half * 1024 + (tl + 1) * 128)
            k.dma("sync", xr[:], X2d[tok, :], reads=[X2], writes=[xr])
            k.op("vector", lambda e, tl=tl: e.scalar_tensor_tensor(zz[:], xr[:], ALPHA, yt[:, tl, :], op0=ALU.mult, op1=ALU.add), reads=[xr, yacc[tl]], writes=[zz])
            layernorm(k, zz, LN[4], LN[5], ob, junk, st, eps)
            k.dma("gpsimd", xout[tok, :], ob[:], reads=[ob], writes=[XO])
    k.finish([XO])
    return nc


DILS = (1, 4, 16)
NEG = -30000.0


def tokview(ap2, d, n, r):
    if d == 1:
        return ap2[:, n * 128:(n + 1) * 128]
    span = 128 * d
    return ap2[:, n * span:(n + 1) * span].rearrange("p (i r) -> p i r", r=d)[:, :, r]


def build_mixer(phases=("pool", "att", "dn"), dbg=False, nh=3, dn_stop=9, nlv=7, sq=1):
    nc = bass.Bass("TRN2", target_bir_lowering=False)
    dt = nc.dram_tensor
    xs = dt("xs", [SEQ, D], F32, kind="ExternalInput").ap()
    pos = dt("pos", [SEQ], I32, kind="ExternalInput").ap()
    w_aq = dt("w_aq", [3, D, 64], F32, kind="ExternalInput").ap()
    w_ak = dt("w_ak", [3, D, 64], F32, kind="ExternalInput").ap()
    w_av = dt("w_av", [3, D, 64], F32, kind="ExternalInput").ap()
    slopes = dt("slopes", [1, 8], F32, kind="ExternalInput").ap()
    w_dq = dt("w_dq", [3, D, 64], F32, kind="ExternalInput").ap()
    w_dk = dt("w_dk", [3, D, 64], F32, kind="ExternalInput").ap()
    w_dv = dt("w_dv", [3, D, 64], F32, kind="ExternalInput").ap()
    w_dg = dt("w_dg", [3, D, 64], F32, kind="ExternalInput").ap()
    w_dbd = dt("w_dbd", [3, D, 2], F32, kind="ExternalInput").ap()
    cw = dt("cw", [3, 3, 64, 4], F32, kind="ExternalInput").ap()
    dnp = dt("dnp", [1, 8], F32, kind="ExternalInput").ap()
    dnw = dt("dnw", [64, 1], F32, kind="ExternalInput").ap()
    w_pi = dt("w_pi", [D, 128], F32, kind="ExternalInput").ap()
    pool_bd = dt("pool_bd", [128, 128], F32, kind="ExternalInput").ap()
    ptab = dt("ptab", [128, 32], F32, kind="ExternalInput").ap()
    mixo = dt("mixo", [512, SEQ], BF16, kind="ExternalOutput").ap()
    XTd = dt("XTs", [D, SEQ], BF16).ap()
    k = KB(nc)
    XT = Buf(XTd, "XT"); MO = Buf(mixo, "mixo")
    c = mk_consts(k)
    identf, identb = c["identf"], c["identb"]
    XTv = XTd.rearrange("(kc p) t -> p kc t", p=128)
    PB = [k.ps([128, 512], F32, "PB%d" % i) for i in range(7)]
    PT = k.ps([128, 1024], BF16, "PT")
    stage = k.sb([128, 8 * 128], F32, "stage")
    eps6 = k.sb([128, 1], F32, "eps6")
    k.op("gpsimd", lambda e: e.memset(eps6[:], 1e-6), writes=[eps6])

    m0 = k.mark()
    xl = [k.sb([128, D], F32, "xl%d" % i) for i in range(2)]
    xlb = k.sb([128, D], BF16, "xlb")
    xTt = [k.sb([128, 8, 128], BF16, "xTt%d" % i) for i in range(2)]
    for ti in range(SEQ // 128):
        tok = slice(ti * 128, (ti + 1) * 128)
        a = xl[ti % 2]; o = xTt[ti % 2]
        k.dma("sync", a[:], xs[tok, :], writes=[a])
        k.op("scalar", lambda e, a=a: e.copy(xlb[:], a[:]), reads=[a], writes=[xlb])
        for kc in range(8):
            k.transpose(PT[:, kc * 128:(kc + 1) * 128], xlb[:, kc * 128:(kc + 1) * 128], identb[:], reads=[xlb, identb], writes=[PT])
        k.op("vector", lambda e, o=o: e.tensor_copy(o[:].rearrange("p a b -> p (a b)"), PT[:]), reads=[PT], writes=[o])
        k.dma("gpsimd", XTv[:, :, tok], o[:], reads=[o], writes=[XT])
    k.release(m0)

    xT = [k.sb([128, 8, 512], BF16, "xT%d" % i) for i in range(2)]
    NTL = SEQ // 512

    m1 = k.mark()
    Wpi = k.sb([128, 8, 128], BF16, "Wpi")
    load_w(k, Wpi, Wpi[:], w_pi.rearrange("(kc p) n -> p kc n", p=128), stage)
    Wbd = k.sb([128, 128], BF16, "Wbd")
    load_w(k, Wbd, Wbd[:].rearrange("p (a n) -> p a n", a=1), pool_bd.rearrange("p (a n) -> p a n", a=1), stage)
    pt = k.sb([128, 32], F32, "ptabs")
    k.dma("sync", pt[:], ptab, writes=[pt])
    xp = k.sb([128, 528], F32, "xp")
    s = [k.sb([128, 528], F32, "s%d" % i) for i in range(4)]
    acc = k.sb([128, 512], F32, "pacc")
    fx = k.sb([128, 16], F32, "pfx")
    accb = k.sb([128, 512], BF16, "paccb")
    pout = [k.sb([128, 512], BF16, "pout%d" % i) for i in range(2)]
    k.op("vector", lambda e: e.memset(xp[:], 0.0), writes=[xp])
    for tl in (range(NTL) if "pool" in phases else []):
        ts_ = slice(tl * 512, (tl + 1) * 512)
        xb = xT[tl % 2]
        k.dma("sync", xb[:], XTv[:, :, ts_], reads=[XT], writes=[xb])
        k.mm(PB[0][:], [(Wpi[:, kc, :], xb[:, kc, :]) for kc in range(8)], reads=[Wpi, xb], writes=[PB[0]])
        if tl > 0:
            k.op("vector", lambda e: e.tensor_copy(xp[:, 0:16], xp[:, 512:528]), reads=[xp], writes=[xp])
        k.op("vector", lambda e: e.tensor_copy(xp[:, 16:528], PB[0][:]), reads=[PB[0], xp], writes=[xp])
        src = xp
        for lv, sh in enumerate((1, 2, 4, 8)):
            lo = 2 * sh - 1
            k.op("vector" if lv % 2 == 0 else "gpsimd",
                 lambda e, lv=lv, sh=sh, lo=lo, src=src: e.tensor_tensor(s[lv][:, lo:528], src[:, lo:528], src[:, lo - sh:528 - sh], op=ALU.add),
                 reads=[src], writes=[s[lv]])
            src = s[lv]
        k.op("vector", lambda e: e.tensor_scalar_mul(acc[:], xp[:, 16:528], pt[:, 0:1]), reads=[xp, pt], writes=[acc])
        for lv in range(4):
            k.op("vector", lambda e, lv=lv: e.scalar_tensor_tensor(acc[:], s[lv][:, 16:528], pt[:, 1 + lv:2 + lv], acc[:], op0=ALU.mult, op1=ALU.add),
                 reads=[s[lv], pt, acc], writes=[acc])
        if tl == 0:
            k.op("vector", lambda e: e.tensor_scalar_mul(fx[:], s[0][:, 16:32], pt[:, 5:6]), reads=[s[0], pt], writes=[fx])
            for lv in range(1, 4):
                k.op("vector", lambda e, lv=lv: e.scalar_tensor_tensor(fx[:], s[lv][:, 16:32], pt[:, 5 + lv:6 + lv], fx[:], op0=ALU.mult, op1=ALU.add),
                     reads=[s[lv], pt, fx], writes=[fx])
            k.op("vector", lambda e: e.tensor_tensor(fx[:], fx[:], pt[:, 16:32], op=ALU.mult), reads=[fx, pt], writes=[fx])
            k.op("vector", lambda e: e.tensor_tensor(acc[:, 0:16], fx[:], xp[:, 16:32], op=ALU.subtract), reads=[fx, xp, acc], writes=[acc])
        k.op("scalar", lambda e: e.copy(accb[:], acc[:]), reads=[acc], writes=[accb])
        k.mm(PB[1][:], [(Wbd[:], accb[:])], reads=[Wbd, accb], writes=[PB[1]])
        po = pout[tl % 2]
        k.op("scalar", lambda e, po=po: e.activation(po[:], PB[1][:], AF.Copy, scale=pt[:, 9:10]), reads=[PB[1], pt], writes=[po])
        k.dma("gpsimd", mixo[384:512, ts_], po[:], reads=[po], writes=[MO])
    k.release(m1)

    qT = k.sb([64, SEQ], BF16, "qT")
    kT = k.sb([64, SEQ], BF16, "kT")
    vT = k.sb([64, SEQ], BF16, "vT")
    Wq = k.sb([128, 8, 64], BF16, "Wq")
    Wk = k.sb([128, 8, 64], BF16, "Wk")
    Wv = k.sb([128, 8, 64], BF16, "Wv")
    shared_mark = k.mark()

    slp = k.sb([128, 8], F32, "slp")
    k.dma("sync", slp[:], slopes.broadcast_to([128, 8]), writes=[slp])
    posrow = k.sb([1, SEQ], BF16, "posrow")
    pi32 = k.sb([1, 2048], I32, "pi32")
    posv = pos.rearrange("(o t) -> o t", o=1)
    for j in range(4):
        k.dma("sync", pi32[:], posv[:, j * 2048:(j + 1) * 2048], writes=[pi32])
        k.op("vector", lambda e, j=j: e.tensor_copy(posrow[:, j * 2048:(j + 1) * 2048], pi32[:]), reads=[pi32], writes=[posrow])
    posk = []
    pki = k.sb([128, 64], I32, "pki")
    for d in DILS:
        pk = k.sb([128, 64], F32, "posk%d" % d)
        if d == 1:
            k.dma("sync", pki[:], pos.rearrange("(n p) -> p n", p=128), writes=[pki], allow_slow_non_contiguous=True)
        else:
            k.dma("sync", pki[:].rearrange("p (n r) -> p n r", r=d), pos.rearrange("(n p r) -> p n r", p=128, r=d), writes=[pki],
                  allow_slow_non_contiguous=True)
        k.op("vector", lambda e, pk=pk: e.tensor_copy(pk[:], pki[:]), reads=[pki], writes=[pk])
        posk.append(pk)
    mtmp = k.sb([128, 128], F32, "mtmp")
    MTp = k.sb([128, 128], BF16, "MTp")
    MTc = k.sb([128, 128], BF16, "MTc")
    k.op("gpsimd", lambda e: e.memset(mtmp[:], 0.0), writes=[mtmp])
    k.op("gpsimd", lambda e: e.affine_select(mtmp[:], mtmp[:], pattern=[[-1, 128]], compare_op=ALU.is_ge, fill=NEG, base=0, channel_multiplier=1),
         reads=[mtmp], writes=[mtmp])
    k.op("vector", lambda e: e.tensor_copy(MTp[:], mtmp[:]), reads=[mtmp], writes=[MTp])
    k.op("gpsimd", lambda e: e.memset(mtmp[:], 0.0), reads=[mtmp], writes=[mtmp])
    k.op("gpsimd", lambda e: e.affine_select(mtmp[:], mtmp[:], pattern=[[1, 128]], compare_op=ALU.is_ge, fill=NEG, base=0, channel_multiplier=-1),
         reads=[mtmp], writes=[mtmp])
    k.op("vector", lambda e: e.tensor_copy(MTc[:], mtmp[:]), reads=[mtmp], writes=[MTc])
    ones_row = k.sb([1, 128], BF16, "ones_row")
    k.op("vector", lambda e: e.memset(ones_row[:], 1.0), writes=[ones_row])
    seld = k.sb([128, 64], F32, "seld")
    k.op("gpsimd", lambda e: e.memset(seld[:], 1.0), writes=[seld])
    k.op("gpsimd", lambda e: e.affine_select(seld[:], seld[:], pattern=[[-1, 64]], compare_op=ALU.is_equal, fill=0.0, base=-64, channel_multiplier=1),
         reads=[seld], writes=[seld])
    vaug = []
    for d in DILS:
        va = k.sb([128, 64, 128], BF16, "vaug%d" % d)
        k.op("gpsimd", lambda e, va=va: e.memset(va[:], 1.0), writes=[va])
        vaug.append(va)
    crow = k.sb([1, SEQ], BF16, "crow")
    biask = [k.sb([128, 64], F32, "biask%d" % d) for d in DILS]
    O = k.sb([128, SEQ], F32, "Oacc")
    pTt = [k.sb([128, 256], BF16, "pTt%d" % i) for i in range(2)]
    rden = k.sb([64, 512], F32, "rden")
    aout = [k.sb([64, 512], BF16, "aout%d" % i) for i in range(2)]
    if dbg:
        dq = dt("dbg_q", [64, SEQ], BF16, kind="ExternalOutput").ap()
        dk = dt("dbg_k", [64, SEQ], BF16, kind="ExternalOutput").ap()
        dv = dt("dbg_v", [64, SEQ], BF16, kind="ExternalOutput").ap()
        dO = dt("dbg_O", [128, SEQ], F32, kind="ExternalOutput").ap()
        dva = dt("dbg_va", [128, 64 * 128], BF16, kind="ExternalOutput").ap()
        dcr = dt("dbg_crow", [1, SEQ], BF16, kind="ExternalOutput").ap()
        dbk = dt("dbg_bk", [128, 64], F32, kind="ExternalOutput").ap()
        dpt = dt("dbg_pt", [128, 256], BF16, kind="ExternalOutput").ap()
    for h in (range(nh) if "att" in phases else []):
        for (W, src) in ((Wq, w_aq), (Wk, w_ak), (Wv, w_av)):
            load_w(k, W, W[:], src[h].rearrange("(kc p) n -> p kc n", p=128), stage)
        for tl in range(NTL):
            ts_ = slice(tl * 512, (tl + 1) * 512)
            xb = xT[tl % 2]
            k.dma("sync", xb[:], XTv[:, :, ts_], reads=[XT], writes=[xb])
            for j, (W, dst, sc) in enumerate(((Wq, qT, 0.125), (Wk, kT, 1.0), (Wv, vT, 1.0))):
                pb = PB[j]
                k.mm(pb[0:64, :], [(W[:, kc, :], xb[:, kc, :]) for kc in range(8)], reads=[W, xb], writes=[pb])
                k.op("scalar" if j != 1 else "vector",
                     (lambda e, pb=pb, dst=dst, sc=sc, ts_=ts_: e.activation(dst[:, ts_], pb[0:64, :], AF.Copy, scale=sc)) if j != 1 else
                     (lambda e, pb=pb, dst=dst, ts_=ts_: e.tensor_copy(dst[:, ts_], pb[0:64, :])),
                     reads=[pb], writes=[dst])
        for di, d in enumerate(DILS):
            for b0 in range(0, 64, 16):
                for bb in range(16):
                    b = b0 + bb
                    n, r = b // d, b % d
                    k.transpose(PT[:, bb * 64:(bb + 1) * 64], tokview(vT[:], d, n, r), identb[0:64, 0:64], reads=[vT, identb], writes=[PT])
                k.op("vector", lambda e, di=di, b0=b0: e.tensor_copy(vaug[di][:, b0:b0 + 16, 0:64], PT[:].rearrange("p (a b) -> p a b", b=64)),
                     reads=[PT], writes=[vaug[di]])
        k.op("vector", lambda e, h=h: e.tensor_scalar_mul(crow[:], posrow[:], slp[0:1, 4 + h:5 + h]), reads=[posrow, slp], writes=[crow])
        for di in range(3):
            k.op("vector", lambda e, di=di, h=h: e.tensor_scalar_mul(biask[di][:], posk[di][:], slp[:, h:h + 1]), reads=[posk[di], slp], writes=[biask[di]])
        blk = 0
        for di, d in enumerate(DILS):
            for b in range(64):
                n, r = b // d, b % d
                ps = PB[3 + blk % 2]
                po = PB[5 + blk % 2]
                pt_ = pTt[blk % 2]
                halves = []
                if n >= 1:
                    halves.append((0, n - 1, MTp))
                halves.append((1, n, MTc))
                for (hi, nk, MT) in halves:
                    k.mm(ps[:, hi * 128:(hi + 1) * 128],
                         [(identb[:], MT[:]), (tokview(kT[:], d, nk, r), tokview(qT[:], d, n, r)), (ones_row[:], tokview(crow[:], d, n, r))],
                         reads=[identb, MT, kT, qT, ones_row, crow], writes=[ps])
                    bk = nk * d + r
                    k.op("scalar", lambda e, hi=hi, ps=ps, pt_=pt_, di=di, bk=bk: e.activation(pt_[:, hi * 128:(hi + 1) * 128], ps[:, hi * 128:(hi + 1) * 128],
                                                                                              AF.Exp, bias=biask[di][:, bk:bk + 1], scale=1.0),
                         reads=[ps, biask[di]], writes=[pt_])
                k.mm(po[:, 0:128], [(vaug[di][:, nk * d + r, :], pt_[:, hi * 128:(hi + 1) * 128]) for (hi, nk, MT) in halves],
                     reads=[vaug[di], pt_], writes=[po])
                ov = tokview(O[:], d, n, r)
                if di == 0:
                    k.op("vector", lambda e, ov=ov, po=po: e.tensor_copy(ov, po[:, 0:128]), reads=[po], writes=[O])
                else:
                    k.op("vector", lambda e, ov=ov, po=po: e.tensor_tensor(ov, ov, po[:, 0:128], op=ALU.add), reads=[po, O], writes=[O])
                blk += 1
        if dbg and h == 0:
            k.dma("sync", dq, qT[:], reads=[qT]); k.dma("sync", dk, kT[:], reads=[kT]); k.dma("sync", dv, vT[:], reads=[vT])
            k.dma("sync", dO, O[:], reads=[O]); k.dma("sync", dva, vaug[0][:].rearrange("p a b -> p (a b)"), reads=[vaug[0]])
            k.dma("sync", dcr, crow[:], reads=[crow]); k.dma("sync", dbk, biask[0][:], reads=[biask[0]])
            k.dma("sync", dpt, pTt[1][:], reads=[pTt[1]])
        for tl in range(NTL):
            ts_ = slice(tl * 512, (tl + 1) * 512)
            k.mm(PB[0][0:64, :], [(seld[:], O[:, ts_])], reads=[seld, O], writes=[PB[0]])
            k.op("vector", lambda e: e.reciprocal(rden[:], PB[0][0:64, :]), reads=[PB[0]], writes=[rden])
            ao = aout[tl % 2]
            k.op("vector", lambda e, ao=ao, ts_=ts_: e.tensor_tensor(ao[:], O[0:64, ts_], rden[:], op=ALU.mult), reads=[O, rden], writes=[ao])
            k.dma("gpsimd", mixo[h * 64:(h + 1) * 64, ts_], ao[:], reads=[ao], writes=[MO])
    k.release(shared_mark)
    if "dn" in phases:
        build_dn(k, nc, locals())
    k.finish([MO])
    return nc


def build_dn(k, nc, L):
    qT, kT, vT, Wq, Wk, Wv = L["qT"], L["kT"], L["vT"], L["Wq"], L["Wk"], L["Wv"]
    xT, PB, PT, stage, identf, identb, eps6 = L["xT"], L["PB"], L["PT"], L["stage"], L["identf"], L["identb"], L["eps6"]
    XTv, XT, MO, mixo = L["XTv"], L["XT"], L["MO"], L["mixo"]
    NTL = SEQ // 512
    NCH = SEQ // 128
    Wg = k.sb([128, 8, 64], BF16, "Wg")
    Wbdn = k.sb([128, 8, 2], BF16, "Wbdn")
    cwt = k.sb([64, 3, 4], F32, "cwt")
    dpt = k.sb([128, 8], F32, "dpt")
    k.dma("sync", dpt[:], L["dnp"].broadcast_to([128, 8]), writes=[dpt])
    nwt = k.sb([64, 1], F32, "nwt")
    k.dma("sync", nwt[:], L["dnw"], writes=[nwt])
    negA = k.sb([128, 4], F32, "negA")
    k.op("scalar", lambda e: e.activation(negA[:], dpt[:, 0:4], AF.Exp), reads=[dpt], writes=[negA])
    k.op("vector", lambda e: e.tensor_scalar_mul(negA[:], negA[:], -1.0), reads=[negA], writes=[negA])
    ones128 = k.sb([128, 128], F32, "ones128")
    k.op("gpsimd", lambda e: e.memset(ones128[:], 1.0), writes=[ones128])
    U = k.sb([128, 128], F32, "Utri")
    k.op("gpsimd", lambda e: e.memset(U[:], 1.0), writes=[U])
    k.op("gpsimd", lambda e: e.affine_select(U[:], U[:], pattern=[[1, 128]], compare_op=ALU.is_ge, fill=0.0, base=0, channel_multiplier=-1),
         reads=[U], writes=[U])
    SL = k.sb([128, 128], F32, "SLm")
    k.op("gpsimd", lambda e: e.memset(SL[:], 1.0), writes=[SL])
    k.op("gpsimd", lambda e: e.affine_select(SL[:], SL[:], pattern=[[-1, 128]], compare_op=ALU.is_gt, fill=0.0, base=0, channel_multiplier=1),
         reads=[SL], writes=[SL])
    Mpos = k.sb([128, 128], F32, "Mpos")
    k.op("gpsimd", lambda e: e.memset(Mpos[:], 0.0), writes=[Mpos])
    k.op("gpsimd", lambda e: e.affine_select(Mpos[:], Mpos[:], pattern=[[-1, 128]], compare_op=ALU.is_ge, fill=-NEG, base=0, channel_multiplier=1),
         reads=[Mpos], writes=[Mpos])
    Mneg = k.sb([128, 128], F32, "Mneg")
    k.op("gpsimd", lambda e: e.memset(Mneg[:], 0.0), writes=[Mneg])
    k.op("gpsimd", lambda e: e.affine_select(Mneg[:], Mneg[:], pattern=[[1, 128]], compare_op=ALU.is_ge, fill=NEG, base=0, channel_multiplier=-1),
         reads=[Mneg], writes=[Mneg])
    sgT = k.sb([64, SEQ], BF16, "sgT")
    rq = [k.sb([64, 515], F32, "rq%d" % i) for i in range(3)]
    ca = [k.sb([64, 512], F32, "ca%d" % i) for i in range(3)]
    sq2 = k.sb([64, 512], F32, "sq2")
    rn = k.sb([64, 512], F32, "rn")
    bdraw = k.sb([128, NCH, 2], F32, "bdraw")
    tm = [k.sb([128, NCH], F32, "tm%d" % i) for i in range(10)]
    beta, gt, G, Gl, eG, bG, dG, gtot, negG, t9 = tm
    X = [k.sb([128, 128], F32, "X%d" % i) for i in range(2)]
    Mb = [k.sb([128, 128], F32, "M%d" % i) for i in range(2)]
    Bb = [k.sb([128, 128], F32, "B%d" % i) for i in range(2)]
    diagG = k.sb([128, 128], F32, "diagG")
    E = k.sb([128, 128], F32, "E")
    Es = k.sb([128, 128], F32, "Es")
    ET = k.sb([128, 128], F32, "ET")
    QKT = k.sb([128, 128], F32, "QKT")
    kdec = k.sb([128, 64], F32, "kdec")
    wT = k.sb([64, 128], F32, "wT")
    S = k.sb([64, 64], F32, "S")
    Sb = k.sb([64, 64], BF16, "Sb")
    vnew = k.sb([128, 64], F32, "vnew")
    o1s = k.sb([128, 64], F32, "o1s")
    o = k.sb([128, 64], F32, "o")
    ojunk = k.sb([128, 64], F32, "ojunk")
    onb = k.sb([128, 64], BF16, "onb")
    so = k.sb([128, 8], F32, "so")
    for h in range(L["nh"]):
        for (W, src) in ((Wq, L["w_dq"]), (Wk, L["w_dk"]), (Wv, L["w_dv"]), (Wg, L["w_dg"]), (Wbdn, L["w_dbd"])):
            load_w(k, W, W[:], src[h].rearrange("(kc p) n -> p kc n", p=128), stage)
        k.dma("sync", cwt[:], L["cw"][h].rearrange("a c j -> c a j"), writes=[cwt])
        for j in range(3):
            k.op("vector", lambda e, j=j: e.memset(rq[j][:, 0:3], 0.0), writes=[rq[j]])
        for tl in range(NTL):
            ts_ = slice(tl * 512, (tl + 1) * 512)
            xb = xT[tl % 2]
            k.dma("sync", xb[:], XTv[:, :, ts_], reads=[XT], writes=[xb])
            for j, (W, dst) in enumerate(((Wq, qT), (Wk, kT), (Wv, vT))):
                pb = PB[j]
                r = rq[j]; a = ca[j]
                k.mm(pb[0:64, :], [(W[:, kc, :], xb[:, kc, :]) for kc in range(8)], reads=[W, xb], writes=[pb])
                if tl > 0:
                    k.op("vector", lambda e, r=r: e.tensor_copy(r[:, 0:3], r[:, 512:515]), reads=[r], writes=[r])
                k.op("scalar", lambda e, r=r, pb=pb: e.copy(r[:, 3:515], pb[0:64, :]), reads=[pb, r], writes=[r])
                eng = "vector"
                k.op(eng, lambda e, r=r, a=a, j=j: e.tensor_scalar_mul(a[:], r[:, 3:515], cwt[:, j, 3:4]), reads=[r, cwt], writes=[a])
                for tap in range(3):
                    k.op(eng, lambda e, r=r, a=a, j=j, tap=tap: e.scalar_tensor_tensor(a[:], r[:, tap:tap + 512], cwt[:, j, tap:tap + 1], a[:], op0=ALU.mult, op1=ALU.add),
                         reads=[r, cwt, a], writes=[a])
                k.op("scalar", lambda e, a=a: e.activation(a[:], a[:], AF.Silu), reads=[a], writes=[a])
                if j < 2:
                    k.op("gpsimd", lambda e, a=a: e.tensor_tensor(sq2[:], a[:], a[:], op=ALU.mult), reads=[a], writes=[sq2])
                    k.mm(PB[4][0:64, :], [(ones128[0:64, 0:64], sq2[:])], reads=[ones128, sq2], writes=[PB[4]])
                    k.op("scalar", lambda e: e.activation(rn[:], PB[4][0:64, :], AF.Sqrt, bias=eps6[0:64, 0:1], scale=1.0), reads=[PB[4], eps6], writes=[rn])
                    k.op("vector", lambda e: e.reciprocal(rn[:], rn[:]), reads=[rn], writes=[rn])
                    if j == 0:
                        k.op("vector", lambda e, a=a, dst=dst, ts_=ts_: e.scalar_tensor_tensor(dst[:, ts_], a[:], 0.125, rn[:], op0=ALU.mult, op1=ALU.mult), reads=[a, rn], writes=[dst])
                    else:
                        k.op("vector", lambda e, a=a, dst=dst, ts_=ts_: e.tensor_tensor(dst[:, ts_], a[:], rn[:], op=ALU.mult), reads=[a, rn], writes=[dst])
                else:
                    k.op("vector", lambda e, a=a, dst=dst, ts_=ts_: e.tensor_copy(dst[:, ts_], a[:]), reads=[a], writes=[dst])
            k.mm(PB[3][0:64, :], [(Wg[:, kc, :], xb[:, kc, :]) for kc in range(8)], reads=[Wg, xb], writes=[PB[3]])
            k.op("scalar", lambda e, ts_=ts_: e.activation(sgT[:, ts_], PB[3][0:64, :], AF.Silu), reads=[PB[3]], writes=[sgT])
            for sub in range(4):
                k.mm(PB[5][:, sub * 2:sub * 2 + 2], [(xb[:, kc, sub * 128:(sub + 1) * 128], Wbdn[:, kc, :]) for kc in range(8)], reads=[xb, Wbdn], writes=[PB[5]])
            k.op("vector", lambda e, tl=tl: e.tensor_copy(bdraw[:, tl * 4:(tl + 1) * 4, :].rearrange("p a b -> p (a b)"), PB[5][:, 0:8]), reads=[PB[5]], writes=[bdraw])
        if L.get("dn_stop", 9) < 1:
            continue
        k.op("scalar", lambda e: e.activation(beta[:], bdraw[:, :, 0], AF.Exp, scale=-1.0), reads=[bdraw], writes=[beta])
        k.op("vector", lambda e: e.tensor_scalar_add(beta[:], beta[:], 1.0), reads=[beta], writes=[beta])
        k.op("vector", lambda e: e.reciprocal(beta[:], beta[:]), reads=[beta], writes=[beta])
        k.op("scalar", lambda e, h=h: e.activation(gt[:], bdraw[:, :, 1], AF.Exp, bias=dpt[:, 4 + h:5 + h], scale=1.0), reads=[bdraw, dpt], writes=[gt])
        k.op("vector", lambda e: e.tensor_scalar_add(gt[:], gt[:], 1.0), reads=[gt], writes=[gt])
        k.op("scalar", lambda e: e.activation(gt[:], gt[:], AF.Ln), reads=[gt], writes=[gt])
        k.op("vector", lambda e, h=h: e.tensor_scalar_mul(gt[:], gt[:], negA[:, h:h + 1]), reads=[gt, negA], writes=[gt])
        k.mm(PB[0][:, 0:NCH], [(U[:], gt[:])], reads=[U, gt], writes=[PB[0]])
        k.mm(PB[0][:, 64:64 + NCH], [(ones128[:], gt[:])], reads=[ones128, gt], writes=[PB[0]])
        k.op("vector", lambda e: e.tensor_copy(G[:], PB[0][:, 0:NCH]), reads=[PB[0]], writes=[G])
        k.op("vector", lambda e: e.tensor_copy(Gl[:], PB[0][:, 64:64 + NCH]), reads=[PB[0]], writes=[Gl])
        k.op("scalar", lambda e: e.activation(eG[:], G[:], AF.Exp), reads=[G], writes=[eG])
        k.op("vector", lambda e: e.tensor_tensor(bG[:], beta[:], eG[:], op=ALU.mult), reads=[beta, eG], writes=[bG])
        k.op("vector", lambda e: e.tensor_tensor(t9[:], Gl[:], G[:], op=ALU.subtract), reads=[Gl, G], writes=[t9])
        k.op("scalar", lambda e: e.activation(dG[:], t9[:], AF.Exp), reads=[t9], writes=[dG])
        k.op("scalar", lambda e: e.activation(gtot[:], Gl[:], AF.Exp), reads=[Gl], writes=[gtot])
        k.op("vector", lambda e: e.tensor_scalar_mul(negG[:], G[:], -1.0), reads=[G], writes=[negG])
        k.op("vector", lambda e: e.memset(S[:], 0.0), reads=[S], writes=[S])
        if L.get("dn_stop", 9) < 2:
            continue
        DS = L.get("dn_stop", 9)
        for n in range(NCH if DS > 8 else 1):
            tk = slice(n * 128, (n + 1) * 128)
            nn = slice(n, n + 1)
            k.transpose(PT[:, 0:64], kT[:, tk], identb[0:64, 0:64], reads=[kT, identb], writes=[PT])
            k.transpose(PT[:, 64:128], vT[:, tk], identb[0:64, 0:64], reads=[vT, identb], writes=[PT])
            X0 = X[0]
            k.op("vector", lambda e, X0=X0, nn=nn: e.tensor_scalar_mul(X0[:, 0:64], PT[:, 64:128], beta[:, nn]), reads=[PT, beta], writes=[X0])
            k.op("vector", lambda e, X0=X0, nn=nn: e.tensor_scalar_mul(X0[:, 64:128], PT[:, 0:64], bG[:, nn]), reads=[PT, bG], writes=[X0])
            k.op("scalar", lambda e, nn=nn: e.activation(kdec[:], PT[:, 0:64], AF.Copy, scale=dG[:, nn]), reads=[PT, dG], writes=[kdec])
            k.mm(PB[1][:, 0:128], [(kT[:, tk], kT[:, tk])], reads=[kT], writes=[PB[1]])
            k.mm(PB[1][:, 128:256], [(kT[:, tk], qT[:, tk])], reads=[kT, qT], writes=[PB[1]])
            k.op("gpsimd", lambda e, nn=nn: e.tensor_scalar_mul(diagG[:], identf[:], G[:, nn]), reads=[identf, G], writes=[diagG])
            k.mm(PB[2][:, 0:128], [(ones128[:], diagG[:]), (identf[:], Mpos[:])], reads=[ones128, diagG, identf, Mpos], writes=[PB[2]])
            k.mm(PB[2][:, 128:256], [(ones128[:], diagG[:]), (identf[:], Mneg[:])], reads=[ones128, diagG, identf, Mneg], writes=[PB[2]])
            k.op("scalar", lambda e, nn=nn: e.activation(E[:], PB[2][:, 0:128], AF.Exp, bias=G[:, nn], scale=-1.0), reads=[PB[2], G], writes=[E])
            k.op("scalar", lambda e, nn=nn: e.activation(ET[:], PB[2][:, 128:256], AF.Exp, bias=negG[:, nn], scale=1.0), reads=[PB[2], negG], writes=[ET])
            if DS < 4:
                continue
            k.op("gpsimd", lambda e: e.tensor_tensor(Es[:], E[:], SL[:], op=ALU.mult), reads=[E, SL], writes=[Es])
            nbeta = t9
            M0 = Mb[0]; B0 = Bb[0]
            k.op("vector", lambda e, M0=M0, nn=nn: e.scalar_tensor_tensor(M0[:], PB[1][:, 0:128], beta[:, nn], Es[:], op0=ALU.mult, op1=ALU.mult),
                 reads=[PB[1], beta, Es], writes=[M0])
            k.op("gpsimd", lambda e, M0=M0: e.tensor_scalar_mul(M0[:], M0[:], -1.0), reads=[M0], writes=[M0])
            k.op("vector", lambda e: e.tensor_tensor(QKT[:], PB[1][:, 128:256], ET[:], op=ALU.mult), reads=[PB[1], ET], writes=[QKT])
            k.transpose(PB[3][:, 0:128], M0[:], identf[:], reads=[M0, identf], writes=[PB[3]])
            k.op("scalar", lambda e, B0=B0: e.copy(B0[:], PB[3][:, 0:128]), reads=[PB[3]], writes=[B0])
            if DS < 5:
                continue
            cur = 0
            for lv in range(L.get('nlv', 7)):
                Mc, Bc, Xc = Mb[cur], Bb[cur], X[cur]
                Mn, Bn, Xn = Mb[1 - cur], Bb[1 - cur], X[1 - cur]
                pb = PB[3 + (lv + 1) % 2]
                k.mm(pb[:, 0:128], [(Bc[:], Xc[:])], reads=[Bc, Xc], writes=[pb])
                k.op("vector", lambda e, Xn=Xn, Xc=Xc, pb=pb: e.tensor_tensor(Xn[:], Xc[:], pb[:, 0:128], op=ALU.add), reads=[Xc, pb], writes=[Xn])
                if lv < 6 and L.get('sq', 1):
                    sqm = L.get('sq', 1)
                    if sqm in (1, 2):
                        k.mm(pb[:, 128:256], [(Bc[:], Mc[:])], reads=[Bc, Mc], writes=[pb])
                    if sqm in (1, 3):
                        k.mm(PB[0][:, 0:128], [(Mc[:], Bc[:])], reads=[Bc, Mc], writes=[PB[0]])
                    if sqm in (1, 2):
                        k.op("scalar", lambda e, Mn=Mn, pb=pb: e.copy(Mn[:], pb[:, 128:256]), reads=[pb], writes=[Mn])
                    if sqm in (1, 3):
                        k.op("vector", lambda e, Bn=Bn: e.tensor_copy(Bn[:], PB[0][:, 0:128]), reads=[PB[0]], writes=[Bn])
                cur = 1 - cur
            Xf = X[cur]
            if DS < 6:
                continue
            k.transpose(PB[5][0:64, 0:128], Xf[:, 64:128], identf[:], reads=[Xf, identf], writes=[PB[5]])
            k.op("scalar", lambda e: e.copy(wT[:], PB[5][0:64, 0:128]), reads=[PB[5]], writes=[wT])
            k.op("gpsimd", lambda e: e.tensor_copy(Sb[:], S[:]), reads=[S], writes=[Sb])
            k.mm(PB[6][:, 0:64], [(wT[:], S[:])], reads=[wT, S], writes=[PB[6]])
            k.mm(PB[6][:, 64:128], [(qT[:, tk], Sb[:])], reads=[qT, Sb], writes=[PB[6]])
            k.op("vector", lambda e, Xf=Xf: e.tensor_tensor(vnew[:], Xf[:, 0:64], PB[6][:, 0:64], op=ALU.subtract), reads=[Xf, PB[6]], writes=[vnew])
            k.op("scalar", lambda e, nn=nn: e.activation(o1s[:], PB[6][:, 64:128], AF.Copy, scale=eG[:, nn]), reads=[PB[6], eG], writes=[o1s])
            k.mm(PB[6][:, 128:192], [(QKT[:], vnew[:])], reads=[QKT, vnew], writes=[PB[6]])
            k.mm(PB[6][0:64, 192:256], [(kdec[:], vnew[:])], reads=[kdec, vnew], writes=[PB[6]])
            k.op("vector", lambda e: e.tensor_tensor(o[:], o1s[:], PB[6][:, 128:192], op=ALU.add), reads=[o1s, PB[6]], writes=[o])
            k.op("vector", lambda e, nn=nn: e.scalar_tensor_tensor(S[:], S[:], gtot[0:64, nn], PB[6][0:64, 192:256], op0=ALU.mult, op1=ALU.add),
                 reads=[S, gtot, PB[6]], writes=[S])
            if DS < 7:
                continue
            k.op("vector", lambda e: e.memset(so[:, 0:1], 0.0), writes=[so])
            k.op("scalar", lambda e: e.activation(ojunk[:], o[:], AF.Square, accum_out=so[:, 0:1]), reads=[o, so], writes=[ojunk, so])
            k.op("scalar", lambda e: e.activation(so[:, 1:2], so[:, 0:1], AF.Sqrt, bias=eps6[:, 0:1], scale=1.0 / 64), reads=[so, eps6], writes=[so])
            k.op("vector", lambda e: e.reciprocal(so[:, 2:3], so[:, 1:2]), reads=[so], writes=[so])
            k.op("vector", lambda e: e.tensor_scalar_mul(onb[:], o[:], so[:, 2:3]), reads=[o, so], writes=[onb])
            k.transpose(PT[0:64, 128:256], onb[:], identb[:], reads=[onb, identb], writes=[PT])
            k.op("vector", lambda e, tk=tk: e.scalar_tensor_tensor(sgT[:, tk], PT[0:64, 128:256], nwt[:, 0:1], sgT[:, tk], op0=ALU.mult, op1=ALU.mult),
                 reads=[PT, nwt, sgT], writes=[sgT])
        k.dma("gpsimd", mixo[192 + h * 64:192 + (h + 1) * 64, :], sgT[:], reads=[sgT], writes=[MO])


OFF_ATT = 0
OFF_DN = 1152
OFF_BETA = 2304
OFF_DECAY = 2310
OFF_GATE = 2316
OFF_POOL = 2700
POOL_WINS = (2, 4, 8, 16)


def _c(a):
    return np.ascontiguousarray(a)


def mixer_inputs(inp, l, xcur, b, hh):
    w_in = inp["w_in"][l]
    ah = [3 * hh + i for i in range(3)]
    m = {"xs": _c(xcur[b]), "pos": _c(inp["positions"][b]).astype(np.int32)}
    m["w_aq"] = _c(np.stack([w_in[:, OFF_ATT + h * 64:OFF_ATT + (h + 1) * 64] for h in ah]))
    m["w_ak"] = _c(np.stack([w_in[:, OFF_ATT + 384 + h * 64:OFF_ATT + 384 + (h + 1) * 64] for h in ah]))
    m["w_av"] = _c(np.stack([w_in[:, OFF_ATT + 768 + h * 64:OFF_ATT + 768 + (h + 1) * 64] for h in ah]))
    sl = [2.0 ** (-8.0 * (h + 1) / 6.0) for h in ah]
    m["slopes"] = np.array([sl + [0.0] + [-s for s in sl] + [0.0]], np.float32)
    m["w_dq"] = _c(np.stack([w_in[:, OFF_DN + h * 64:OFF_DN + (h + 1) * 64] for h in ah]))
    m["w_dk"] = _c(np.stack([w_in[:, OFF_DN + 384 + h * 64:OFF_DN + 384 + (h + 1) * 64] for h in ah]))
    m["w_dv"] = _c(np.stack([w_in[:, OFF_DN + 768 + h * 64:OFF_DN + 768 + (h + 1) * 64] for h in ah]))
    m["w_dg"] = _c(np.stack([w_in[:, OFF_GATE + h * 64:OFF_GATE + (h + 1) * 64] for h in ah]))
    m["w_dbd"] = _c(np.stack([np.stack([w_in[:, OFF_BETA + h], w_in[:, OFF_DECAY + h]], -1) for h in ah]))
    cwl = inp["conv_w"][l]
    m["cw"] = _c(np.stack([np.stack([cwl[:, a * 384 + h * 64:a * 384 + (h + 1) * 64].T for a in range(3)]) for h in ah]))
    dnp = np.zeros((1, 8), np.float32)
    for i, h in enumerate(ah):
        dnp[0, i] = inp["a_log"][l][h]
        dnp[0, 4 + i] = inp["dt_bias"][l][h]
    m["dnp"] = dnp
    m["dnw"] = _c(inp["dn_norm_w"][l].reshape(64, 1))
    m["w_pi"] = _c(w_in[:, OFF_POOL + 2 * hh * 64:OFF_POOL + (2 * hh + 2) * 64])
    bd = np.zeros((128, 128), np.float32)
    bd[0:64, 0:64] = inp["pool_w"][l][2 * hh]
    bd[64:128, 64:128] = inp["pool_w"][l][2 * hh + 1]
    m["pool_bd"] = bd
    pt = np.zeros((128, 32), np.float32)
    for p in range(128):
        g = 2 * hh + p // 64
        w = POOL_WINS[g]
        pt[p, 0] = -1.0
        pt[p, 1 + g] = 1.0 / w
        pt[p, 5 + g] = 1.0
        pt[p, 16:32] = [1.0 / min(t + 1, w) for t in range(16)]
    pt[:, 9] = inp["pool_scale"][l][2 * hh * 64:(2 * hh + 2) * 64]
    m["ptab"] = pt
    return m


def assemble_mixT(parts):
    out = np.empty((1024, SEQ), parts[0].dtype)
    for hh in range(2):
        out[hh * 192:(hh + 1) * 192] = parts[hh][0:192]
        out[384 + hh * 192:384 + (hh + 1) * 192] = parts[hh][192:384]
        out[768 + hh * 128:768 + (hh + 1) * 128] = parts[hh][384:512]
    return out


def rest_inputs(inp, l, xcur, mixT_b, b, hh):
    tok = slice(hh * NTOK, (hh + 1) * NTOK)
    m = {"xin": _c(xcur[b][tok]), "mixT": _c(mixT_b[:, tok]), "memT": _c(inp["mem"][b].T),
         "w_out": inp["w_out"][l], "xq": inp["xq_w"][l], "xk": inp["xk_w"][l], "xv": inp["xv_w"][l], "xo": inp["xo_w"][l],
         "lnp": _c(np.stack([inp["ln_mix_g"][l], inp["ln_mix_b"][l], inp["ln_x_g"][l], inp["ln_x_b"][l], inp["ln_ffn_g"][l], inp["ln_ffn_b"][l]]))}
    j = l // 2
    if l % 2 == 1:
        m.update({"rw": _c(inp["router_w"][j].reshape(8, 128, NEXP).transpose(1, 0, 2)), "w1": inp["moe_w1"][j], "w3": inp["moe_w3"][j], "w2": inp["moe_w2"][j]})
    else:
        m.update({"w1": inp["ffn_w1"][j:j + 1], "w3": inp["ffn_w3"][j:j + 1], "w2": inp["ffn_w2"][j:j + 1]})
    return m


def kernel(**inp):
    inp = {k_: np.asarray(v) for k_, v in inp.items()}
    xcur = inp["x"]
    for l in range(4):
        ncm = build_mixer()
        maps = [mixer_inputs(inp, l, xcur, c // 2, c % 2) for c in range(8)]
        res = run_bass_kernel_spmd(ncm, maps, core_ids=list(range(8)))
        mixT = [assemble_mixT([res.results[2 * b]["mixo"], res.results[2 * b + 1]["mixo"]]) for b in range(4)]
        ncr = build_rest(l % 2 == 1)
        maps = [rest_inputs(inp, l, xcur, mixT[c // 2], c // 2, c % 2) for c in range(8)]
        res = run_bass_kernel_spmd(ncr, maps, core_ids=list(range(8)))
        xcur = np.stack([np.concatenate([res.results[2 * b]["xout"], res.results[2 * b + 1]["xout"]], 0) for b in range(4)])
    return xcur.astype(np.float32)
```
